# Optimizing a Trainium2 kernel written in Bass

```python
import jax, jax.numpy as jnp
from jax import lax
import numpy as np

D_MODEL = 1024
BATCH = 2
SEQ = 16384
DEPTH = 1

R_HEADS = 8
R_HEAD_DIM = 64
R_WIDTH = R_HEADS * R_HEAD_DIM
DECAY_LORA = 64
ICLR_LORA = 64
GATE_LORA = 160
LNX_EPS = 64e-5
R_COLS = 3 * R_WIDTH + DECAY_LORA + ICLR_LORA + GATE_LORA
N_Q_HEADS = 8
N_KV_HEADS = 2
GQA = N_Q_HEADS // N_KV_HEADS
HEAD_DIM = 64
N_WIDTH = N_Q_HEADS * HEAD_DIM
KV_WIDTH = N_KV_HEADS * HEAD_DIM
N_COLS = N_WIDTH + 6 * KV_WIDTH + 3 * N_Q_HEADS
CMP_BLOCK = 32
CMP_STRIDE = 16
CMP_HIDDEN = 256
SEL_BLOCK = 64
SEL_TOPK = 16
WINDOW = 512
Q_BLOCK = 128
ROPE_THETA = 500000.0
ROT_DIM = HEAD_DIM // 4
IN_COLS = R_COLS + N_COLS
MIX_WIDTH = R_WIDTH + N_WIDTH
D_FF = ((8 * D_MODEL + 3 * 256 - 1) // (3 * 256)) * 256
NORM_EPS = 1e-6
MASK = -1e30
FORCE = 1e6

kernel_name = "hymba_rwkv7_nsa_hybrid_layer"


def rms_norm(x, g):
    xf = x.astype(jnp.float32)
    y = xf * lax.rsqrt(jnp.mean(xf * xf, axis=-1, keepdims=True) + NORM_EPS)
    return (y * g.astype(jnp.float32)).astype(x.dtype)


def partial_rope(x, pos):
    half = ROT_DIM // 2
    inv_freq = jnp.power(jnp.float32(ROPE_THETA), -jnp.arange(half, dtype=jnp.float32) * 2.0 / ROT_DIM)
    ang = pos.astype(jnp.float32)[..., None] * inv_freq
    cos = jnp.cos(ang)[:, :, None, :]
    sin = jnp.sin(ang)[:, :, None, :]
    xf = x.astype(jnp.float32)
    x1 = xf[..., :half]
    x2 = xf[..., half:ROT_DIM]
    out = jnp.concatenate([x1 * cos - x2 * sin, x2 * cos + x1 * sin, xf[..., ROT_DIM:]], axis=-1)
    return out.astype(x.dtype)


def rwkv7_group(cols, mu_shift, w0, w2, a0, a2, g2, k_k, k_a, r_k, lnx_w, lnx_b):
    B, S, _ = cols.shape
    f32 = jnp.float32
    prev = jnp.pad(cols, ((0, 0), (1, 0), (0, 0)))[:, :-1]
    cols = cols + (prev - cols) * mu_shift
    o = 0
    r = cols[..., o:o + R_WIDTH]; o += R_WIDTH
    k = cols[..., o:o + R_WIDTH]; o += R_WIDTH
    v = cols[..., o:o + R_WIDTH]; o += R_WIDTH
    w_lat = cols[..., o:o + DECAY_LORA]; o += DECAY_LORA
    a_lat = cols[..., o:o + ICLR_LORA]; o += ICLR_LORA
    g_lat = cols[..., o:o + GATE_LORA]

    w = -jax.nn.softplus(-(w0 + jnp.tanh(w_lat) @ w2).astype(f32)) - 0.5
    decay = jnp.exp(-jnp.exp(w))
    a = jax.nn.sigmoid((a0 + a_lat @ a2).astype(f32))
    g = jax.nn.sigmoid(g_lat) @ g2
    kk = (k * k_k).astype(f32)
    k_mod = k.astype(f32) * (1.0 + (a - 1.0) * k_a.astype(f32))

    heads = lambda t: t.astype(f32).reshape(B, S, R_HEADS, R_HEAD_DIM)
    kk = heads(kk)
    kk = kk / jnp.maximum(jnp.sqrt(jnp.sum(kk * kk, axis=-1, keepdims=True)), 1e-12)
    rh, kh, vh, ah, wh = heads(r), heads(k_mod), heads(v), heads(a), heads(decay)

    def step(state, inp):
        r_t, w_t, k_t, v_t, an_t, b_t = inp
        sa = jnp.einsum('bhvk,bhk->bhv', state, an_t)
        state = state * w_t[:, :, None, :] + sa[..., None] * b_t[:, :, None, :] + v_t[..., None] * k_t[:, :, None, :]
        return state, jnp.einsum('bhvk,bhk->bhv', state, r_t)

    seq_first = lambda t: jnp.moveaxis(t, 1, 0)
    state0 = jnp.zeros((B, R_HEADS, R_HEAD_DIM, R_HEAD_DIM), f32)
    xs = (seq_first(rh), seq_first(wh), seq_first(kh), seq_first(vh), seq_first(-kk), seq_first(kk * ah))
    _, y = lax.scan(step, state0, xs)
    y = jnp.moveaxis(y, 0, 1)

    mean = jnp.mean(y, axis=-1, keepdims=True)
    var = jnp.mean(jnp.square(y - mean), axis=-1, keepdims=True)
    y = ((y - mean) * lax.rsqrt(var + LNX_EPS)).reshape(B, S, R_WIDTH) * lnx_w + lnx_b
    bonus = jnp.sum(rh * kh * r_k.astype(f32), axis=-1, keepdims=True) * vh
    y = (y + bonus.reshape(B, S, R_WIDTH)) * g
    return y.astype(cols.dtype)


def compress(t, pe, w1, b1, w2):
    B, Hk, S, D = t.shape
    n_blk = S // CMP_STRIDE
    n_sub = CMP_BLOCK // CMP_STRIDE
    n_cmp = n_blk - n_sub + 1
    blocks = t.reshape(B, Hk, n_blk, CMP_STRIDE, D)
    win = jnp.concatenate([blocks[:, :, i:i + n_cmp] for i in range(n_sub)], axis=3)
    win = (win + pe).reshape(B, Hk, n_cmp, CMP_BLOCK * D)
    return jax.nn.silu(win @ w1 + b1) @ w2


def nsa_group(cols, positions, pe_k, wk1, bk1, wk2, pe_v, wv1, bv1, wv2):
    B, S, _ = cols.shape
    f32 = jnp.float32
    scale = HEAD_DIM ** -0.5
    q = partial_rope(cols[..., :N_WIDTH].reshape(B, S, N_Q_HEADS, HEAD_DIM), positions)
    q = q.reshape(B, S, N_KV_HEADS, GQA, HEAD_DIM).transpose(0, 2, 3, 1, 4)

    def kv(i, rope):
        o = N_WIDTH + i * KV_WIDTH
        t = cols[..., o:o + KV_WIDTH].reshape(B, S, N_KV_HEADS, HEAD_DIM)
        if rope:
            t = partial_rope(t, positions)
        return t.transpose(0, 2, 1, 3)

    k_cmp, v_cmp = kv(0, False), kv(1, False)
    k_slc, v_slc = kv(2, True), kv(3, True)
    k_win, v_win = kv(4, True), kv(5, True)
    gates = jax.nn.sigmoid(cols[..., N_WIDTH + 6 * KV_WIDTH:].reshape(B, S, N_KV_HEADS, GQA, 3))
    gates = gates.transpose(0, 2, 3, 1, 4)

    kc = compress(k_cmp, pe_k, wk1, bk1, wk2)
    vc = compress(v_cmp, pe_v, wv1, bv1, wv2)
    n_cmp = kc.shape[2]
    pos_c = positions[:, CMP_BLOCK - 1::CMP_STRIDE][:, :n_cmp]
    kc = partial_rope(kc.transpose(0, 2, 1, 3), pos_c).transpose(0, 2, 1, 3)
    cmp_end = jnp.arange(n_cmp) * CMP_STRIDE + CMP_BLOCK - 1

    n_sel = S // SEL_BLOCK
    n_top = min(SEL_TOPK, n_sel)
    ks_blocks = k_slc.reshape(B, N_KV_HEADS, n_sel, SEL_BLOCK, HEAD_DIM)
    vs_blocks = v_slc.reshape(B, N_KV_HEADS, n_sel, SEL_BLOCK, HEAD_DIM)
    c_start = jnp.arange(n_cmp)[:, None] * CMP_STRIDE
    s_start = jnp.arange(n_sel)[None, :] * SEL_BLOCK
    cover = ((c_start < s_start + SEL_BLOCK) & (c_start + CMP_BLOCK > s_start)).astype(f32)
    sel_ids = jnp.arange(n_sel)
    bidx = jnp.arange(B)[:, None, None, None]
    hidx = jnp.arange(N_KV_HEADS)[None, :, None, None]

    kw_pad = jnp.pad(k_win, ((0, 0), (0, 0), (WINDOW, 0), (0, 0)))
    vw_pad = jnp.pad(v_win, ((0, 0), (0, 0), (WINDOW, 0), (0, 0)))

    def q_block(qb):
        start = qb * Q_BLOCK
        t = start + jnp.arange(Q_BLOCK)
        qq = lax.dynamic_slice_in_dim(q, start, Q_BLOCK, axis=3)
        gg = lax.dynamic_slice_in_dim(gates, start, Q_BLOCK, axis=3)

        s = jnp.einsum('bkgqd,bknd->bkgqn', qq, kc).astype(f32) * scale
        valid = cmp_end[None, :] <= t[:, None]
        p = jax.nn.softmax(jnp.where(valid, s, MASK), axis=-1)
        p = jnp.where(valid, p, 0.0)
        o_cmp = jnp.einsum('bkgqn,bknd->bkgqd', p.astype(vc.dtype), vc)

        imp = jnp.sum(p, axis=2) @ cover
        cur = t // SEL_BLOCK
        forced = (sel_ids[None] == 0) | (sel_ids[None] == cur[:, None]) | (sel_ids[None] == cur[:, None] - 1)
        score = jnp.where(forced, FORCE, jnp.where(sel_ids[None] <= cur[:, None], imp, MASK))
        _, idx = lax.top_k(score, n_top)
        k_sel = ks_blocks[bidx, hidx, idx]
        v_sel = vs_blocks[bidx, hidx, idx]
        tok = idx[..., None] * SEL_BLOCK + jnp.arange(SEL_BLOCK)
        valid = (tok <= t[:, None, None])[:, :, None]
        s = jnp.einsum('bkgqd,bkqjld->bkgqjl', qq, k_sel).astype(f32) * scale
        s = jnp.where(valid, s, MASK).reshape(B, N_KV_HEADS, GQA, Q_BLOCK, n_top * SEL_BLOCK)
        p = jax.nn.softmax(s, axis=-1).reshape(B, N_KV_HEADS, GQA, Q_BLOCK, n_top, SEL_BLOCK)
        o_sel = jnp.einsum('bkgqjl,bkqjld->bkgqd', p.astype(v_sel.dtype), v_sel)

        kw = lax.dynamic_slice_in_dim(kw_pad, start, WINDOW + Q_BLOCK, axis=2)
        vw = lax.dynamic_slice_in_dim(vw_pad, start, WINDOW + Q_BLOCK, axis=2)
        kpos = start - WINDOW + jnp.arange(WINDOW + Q_BLOCK)
        valid = (kpos[None] <= t[:, None]) & (kpos[None] > t[:, None] - WINDOW) & (kpos[None] >= 0)
        s = jnp.einsum('bkgqd,bksd->bkgqs', qq, kw).astype(f32) * scale
        p = jax.nn.softmax(jnp.where(valid, s, MASK), axis=-1)
        o_win = jnp.einsum('bkgqs,bksd->bkgqd', p.astype(vw.dtype), vw)

        o = gg[..., 0:1] * o_cmp + gg[..., 1:2] * o_sel + gg[..., 2:3] * o_win
        return o.astype(cols.dtype)

    out = lax.map(q_block, jnp.arange(S // Q_BLOCK))
    out = jnp.moveaxis(out, 0, 3).reshape(B, N_KV_HEADS, GQA, S, HEAD_DIM)
    return out.transpose(0, 3, 1, 2, 4).reshape(B, S, N_WIDTH)


def swiglu(x, w_gate, w_up, w_down):
    return (jax.nn.silu(x @ w_gate) * (x @ w_up)) @ w_down


def setup_inputs(seed: int = 0) -> dict:
    key = jax.random.key(seed)
    ks = iter(jax.random.split(key, 40))
    f32 = jnp.float32
    L = DEPTH
    nrm = lambda shape, s: jax.random.normal(next(ks), shape, f32) * s
    uni = lambda shape, lo, hi: jax.random.uniform(next(ks), shape, f32, lo, hi)
    x = nrm((BATCH, SEQ, D_MODEL), 1.0)
    positions = (jax.random.randint(next(ks), (BATCH, 1), 0, 4096, jnp.int32)
                 + jnp.arange(SEQ, dtype=jnp.int32)[None, :])
    return {
        "x": x,
        "positions": positions,
        "norm_mix": 1.0 + nrm((L, D_MODEL), 0.02),
        "w_in": nrm((L, D_MODEL, IN_COLS), D_MODEL ** -0.5),
        "mu_shift": uni((L, R_COLS), 0.0, 1.0),
        "w0": uni((L, R_WIDTH), -6.0, -1.0),
        "w2": nrm((L, DECAY_LORA, R_WIDTH), 0.1),
        "a0": nrm((L, R_WIDTH), 0.1),
        "a2": nrm((L, ICLR_LORA, R_WIDTH), 0.1),
        "g2": nrm((L, GATE_LORA, R_WIDTH), GATE_LORA ** -0.5),
        "k_k": 0.85 + nrm((L, R_WIDTH), 0.05),
        "k_a": 1.0 + nrm((L, R_WIDTH), 0.05),
        "r_k": nrm((L, R_HEADS, R_HEAD_DIM), 0.1),
        "lnx_w": 1.0 + nrm((L, R_WIDTH), 0.02),
        "lnx_b": nrm((L, R_WIDTH), 0.01),
        "pe_k": nrm((L, CMP_BLOCK, HEAD_DIM), 0.1),
        "wk1": nrm((L, CMP_BLOCK * HEAD_DIM, CMP_HIDDEN), (CMP_BLOCK * HEAD_DIM) ** -0.5),
        "bk1": nrm((L, CMP_HIDDEN), 0.01),
        "wk2": nrm((L, CMP_HIDDEN, HEAD_DIM), CMP_HIDDEN ** -0.5),
        "pe_v": nrm((L, CMP_BLOCK, HEAD_DIM), 0.1),
        "wv1": nrm((L, CMP_BLOCK * HEAD_DIM, CMP_HIDDEN), (CMP_BLOCK * HEAD_DIM) ** -0.5),
        "bv1": nrm((L, CMP_HIDDEN), 0.01),
        "wv2": nrm((L, CMP_HIDDEN, HEAD_DIM), CMP_HIDDEN ** -0.5),
        "w_out": nrm((L, MIX_WIDTH, D_MODEL), MIX_WIDTH ** -0.5),
        "norm_ffn": 1.0 + nrm((L, D_MODEL), 0.02),
        "w_gate": nrm((L, D_MODEL, D_FF), D_MODEL ** -0.5),
        "w_up": nrm((L, D_MODEL, D_FF), D_MODEL ** -0.5),
        "w_down": nrm((L, D_FF, D_MODEL), D_FF ** -0.5),
        "norm_final": 1.0 + nrm((D_MODEL,), 0.02),
    }


def reference(x, positions, norm_mix, w_in, mu_shift, w0, w2, a0, a2, g2, k_k, k_a, r_k, lnx_w, lnx_b,
              pe_k, wk1, bk1, wk2, pe_v, wv1, bv1, wv2, w_out, norm_ffn, w_gate, w_up, w_down, norm_final):
    h = x
    for l in range(DEPTH):
        xn = rms_norm(h, norm_mix[l])
        proj = xn @ w_in[l]
        y_r = rwkv7_group(proj[..., :R_COLS], mu_shift[l], w0[l], w2[l], a0[l], a2[l], g2[l],
                          k_k[l], k_a[l], r_k[l], lnx_w[l], lnx_b[l])
        y_n = nsa_group(proj[..., R_COLS:], positions, pe_k[l], wk1[l], bk1[l], wk2[l],
                        pe_v[l], wv1[l], bv1[l], wv2[l])
        h = h + jnp.concatenate([y_r, y_n], axis=-1) @ w_out[l]
        h = h + swiglu(rms_norm(h, norm_ffn[l]), w_gate[l], w_up[l], w_down[l])
    return rms_norm(h, norm_final)
```

```python
from contextlib import ExitStack
import numpy as np
import ml_dtypes
import concourse.bass as bass
import concourse.mybir as mybir
from concourse.bass_utils import run_bass_kernel_spmd

F32 = mybir.dt.float32
BF16 = mybir.dt.bfloat16
I32 = mybir.dt.int32
AF = mybir.ActivationFunctionType
ALU = mybir.AluOpType
AX = mybir.AxisListType
NPBF = ml_dtypes.bfloat16


class Prog:
    ENG = ("pe", "act", "dve", "pool", "sp")

    def __init__(self, nc):
        self.nc = nc
        self.st = ExitStack()
        self.ops = {e: [] for e in self.ENG}
        self.count = {e: 0 for e in self.ENG}
        self.waited = {e: {} for e in self.ENG}
        self.lastw = {}
        self.readers = {}
        self.dma_cnt = {}
        self.semnames = ["e:" + e for e in self.ENG if e != "sp"]
        self.sems = {}
        self.nops = 0

    def sb(self, name, shape, dt):
        return self.st.enter_context(self.nc.sbuf_tensor(name, list(shape), dt))

    def ps(self, name, shape, dt):
        return self.st.enter_context(self.nc.psum_tensor(name, list(shape), dt))

    def _need(self, eng, reads, writes):
        own = "e:" + eng
        need = {}

        def add(sv, raw):
            s, v = sv
            if s == own and not raw and eng == "pe":
                return
            if need.get(s, 0) < v:
                need[s] = v

        for k in reads:
            if k in self.lastw:
                add(self.lastw[k], True)
        for k in writes:
            if k in self.lastw:
                add(self.lastw[k], False)
            for s, v in self.readers.get(k, {}).items():
                add((s, v), False)
        w = self.waited[eng]
        waits = []
        for s, v in need.items():
            if w.get(s, 0) < v:
                w[s] = v
                waits.append((s, v))
        return waits

    def _mark(self, sv, reads, writes):
        s, v = sv
        for k in reads:
            self.readers.setdefault(k, {})[s] = v
        for k in writes:
            self.lastw[k] = sv
            self.readers[k] = {}

    psum_keys = frozenset()

    def op(self, eng, fn, reads=(), writes=()):
        writes = list(writes) + [k for k in reads if k in self.psum_keys]
        waits = self._need(eng, reads, writes)
        self.count[eng] += 1
        sv = ("e:" + eng, self.count[eng])
        self.ops[eng].append((waits, fn, sv[0], 1))
        self._mark(sv, reads, writes)
        self.nops += 1

    def dma(self, q, out, in_, reads=(), writes=(), semkey=None, **kw):
        if semkey is None:
            semkey = (list(writes) + list(reads))[0]
        sname = "d:" + str(semkey)
        if sname not in self.dma_cnt:
            self.dma_cnt[sname] = 0
            self.semnames.append(sname)
        waits = self._need(q, reads, writes)
        self.dma_cnt[sname] += 16
        sv = (sname, self.dma_cnt[sname])
        self.ops[q].append((waits, lambda e: e.dma_start(out=out, in_=in_, **kw), sname, 16))
        self._mark(sv, reads, writes)
        self.nops += 1

    def mm(self, out, lhsT, rhs, start=True, stop=True, reads=(), writes=()):
        self.op("pe", lambda e: e.matmul(out, lhsT, rhs, start=start, stop=stop), reads, writes)

    def tr(self, out, in_, ident, reads=(), writes=()):
        self.op("pe", lambda e: e.transpose(out, in_, ident), reads, writes)

    def act(self, out, in_, func, reads=(), writes=(), **kw):
        self.op("act", lambda e: e.activation(out, in_, func, **kw), reads, writes)

    def cp(self, eng, out, in_, reads=(), writes=()):
        if eng == "act":
            self.op("act", lambda e: e.copy(out, in_), reads, writes)
        else:
            self.op(eng, lambda e: e.tensor_copy(out, in_), reads, writes)

    def ts(self, eng, out, in0, s1, s2, op0, op1=None, reads=(), writes=(), **kw):
        if op1 is None:
            self.op(eng, lambda e: e.tensor_scalar(out, in0, s1, s2, op0, **kw), reads, writes)
        else:
            self.op(eng, lambda e: e.tensor_scalar(out, in0, s1, s2, op0, op1, **kw), reads, writes)

    def tt(self, eng, out, in0, in1, op, reads=(), writes=()):
        self.op(eng, lambda e: e.tensor_tensor(out, in0, in1, op), reads, writes)

    def stt(self, eng, out, in0, scalar, in1, op0, op1, reads=(), writes=()):
        self.op(eng, lambda e: e.scalar_tensor_tensor(out, in0, scalar, in1, op0, op1), reads, writes)

    def memset(self, eng, ap, val, writes=()):
        self.op(eng, lambda e: e.memset(ap, val), (), writes)

    def recip(self, out, in_, reads=(), writes=()):
        self.op("dve", lambda e: e.reciprocal(out, in_), reads, writes)

    def build(self, final_q="sp"):
        nc = self.nc
        fin = [(s, v) for s, v in self.dma_cnt.items()]
        with ExitStack() as st:
            for i, name in enumerate(self.semnames):
                self.sems[name] = st.enter_context(nc.semaphore("s%d" % i))
            block = st.enter_context(nc.Block())

            def replay(engname):
                def f(e):
                    for waits, fn, sname, inc in self.ops[engname]:
                        for s, v in waits:
                            e.wait_ge(self.sems[s], v)
                        fn(e).then_inc(self.sems[sname], inc)
                    if engname == final_q:
                        for s, v in fin:
                            e.wait_ge(self.sems[s], v)
                return f

            block.tensor(replay("pe"))
            block.scalar(replay("act"))
            block.vector(replay("dve"))
            block.gpsimd(replay("pool"))
            block.sync(replay("sp"))
        self.st.close()
NTOK = 4096
IN_COLS = 3128
DFF = 2816


def build_l1():
    nc = bass.Bass("TRN2", target_bir_lowering=False)
    xT = nc.dram_tensor("xT", [1024, NTOK], F32, kind="ExternalInput").ap()
    x = nc.dram_tensor("x", [NTOK, 1024], F32, kind="ExternalInput").ap()
    w = nc.dram_tensor("w", [1024, IN_COLS], F32, kind="ExternalInput").ap()
    g = nc.dram_tensor("g", [128, 8], F32, kind="ExternalInput").ap()
    proj = nc.dram_tensor("proj", [NTOK, IN_COLS], F32, kind="ExternalOutput").ap()
    P = Prog(nc)
    wb = P.sb("wb", [128, 8, IN_COLS], BF16)
    gs = P.sb("gs", [128, 8], F32)
    stg = [P.sb("stg%d" % i, [128, IN_COLS], F32) for i in range(2)]
    P.dma("sp", gs[:], g, writes=["gs"])
    for c in range(8):
        k = "stg%d" % (c % 2)
        P.dma("sp" if c % 2 == 0 else "pool", stg[c % 2][:], w[c * 128:(c + 1) * 128, :], writes=[k])
        P.ts("dve" if c % 2 == 0 else "pool", wb[:, c, :], stg[c % 2][:], gs[:, c:c + 1], None, ALU.mult,
             reads=[k, "gs"], writes=["wb"])
    xTf = [P.sb("xTf%d" % i, [128, 8, 512], F32) for i in range(2)]
    xTb = [P.sb("xTb%d" % i, [128, 8, 512], BF16) for i in range(2)]
    xt = [P.sb("xt%d" % i, [128, 1024], F32) for i in range(2)]
    junk = P.sb("junk", [128, 1024], F32)
    ss = [P.sb("ss%d" % i, [128, 4], F32) for i in range(2)]
    ob = [P.sb("ob%d" % i, [128, IN_COLS], F32) for i in range(2)]
    pp = [P.ps("pp%d" % i, [128, 512], F32) for i in range(4)]
    cnt = 0
    for T in range(NTOK // 512):
        sl = T % 2
        P.dma("sp", xTf[sl][:], xT[:, T * 512:(T + 1) * 512].rearrange("(c p) t -> p c t", p=128),
              writes=["xTf%d" % sl])
        P.cp("dve", xTb[sl][:, 0:4, :], xTf[sl][:, 0:4, :], reads=["xTf%d" % sl], writes=["xTb%da" % sl])
        P.cp("pool", xTb[sl][:, 4:8, :], xTf[sl][:, 4:8, :], reads=["xTf%d" % sl], writes=["xTb%db" % sl])
        for s in range(4):
            k = T * 4 + s
            b = k % 2
            tok0 = k * 128
            P.dma("pool", xt[b][:], x[tok0:tok0 + 128, :], writes=["xt%d" % b])
            sk = "ss%d" % b
            P.memset("pool", ss[b][:], 0.0, writes=[sk])
            P.act(junk[:], xt[b][:], AF.Square, reads=["xt%d" % b, sk], writes=["junk", sk], accum_out=ss[b][:, 0:1])
            P.act(ss[b][:, 1:2], ss[b][:, 0:1], AF.Sqrt, reads=[sk], writes=[sk], bias=1e-6, scale=1.0 / 1024)
            P.recip(ss[b][:, 2:3], ss[b][:, 1:2], reads=[sk], writes=[sk])
            for n in range(7):
                n0 = n * 512
                nw = min(512, IN_COLS - n0)
                bk = cnt % 4
                cnt += 1
                for c in range(8):
                    P.mm(pp[bk][:, 0:nw], xTb[sl][:, c, s * 128:(s + 1) * 128], wb[:, c, n0:n0 + nw],
                         start=(c == 0), stop=(c == 7),
                         reads=["xTb%da" % sl, "xTb%db" % sl, "wb"], writes=["pp%d" % bk])
                if n % 2 == 0:
                    P.ts("dve", ob[b][:, n0:n0 + nw], pp[bk][:, 0:nw], ss[b][:, 2:3], None, ALU.mult,
                         reads=["pp%d" % bk, sk], writes=["ob%d_%d" % (b, n)])
                else:
                    P.act(ob[b][:, n0:n0 + nw], pp[bk][:, 0:nw], AF.Copy, reads=["pp%d" % bk, sk],
                          writes=["ob%d_%d" % (b, n)], scale=ss[b][:, 2:3])
            P.dma("sp", proj[tok0:tok0 + 128, :], ob[b][:], reads=["ob%d_%d" % (b, n) for n in range(7)],
                  writes=["proj"], semkey="ob%d" % b)
    P.build()
    return nc


def build_l4():
    nc = bass.Bass("TRN2", target_bir_lowering=False)
    yT = nc.dram_tensor("yT", [1024, NTOK], BF16, kind="ExternalInput").ap()
    x = nc.dram_tensor("x", [NTOK, 1024], F32, kind="ExternalInput").ap()
    wo = nc.dram_tensor("wo", [1024, 1024], F32, kind="ExternalInput").ap()
    wg = nc.dram_tensor("wg", [1024, DFF], F32, kind="ExternalInput").ap()
    wu = nc.dram_tensor("wu", [1024, DFF], F32, kind="ExternalInput").ap()
    wd = nc.dram_tensor("wd", [DFF, 1024], F32, kind="ExternalInput").ap()
    gf = nc.dram_tensor("gf", [1, 1024], F32, kind="ExternalInput").ap()
    gl = nc.dram_tensor("gl", [1, 1024], F32, kind="ExternalInput").ap()
    idn = nc.dram_tensor("idn", [128, 128], BF16, kind="ExternalInput").ap()
    out = nc.dram_tensor("out", [NTOK, 1024], F32, kind="ExternalOutput").ap()
    P = Prog(nc)
    wob = P.sb("wob", [128, 8, 1024], BF16)
    wgb = P.sb("wgb", [128, 8, DFF], BF16)
    wub = P.sb("wub", [128, 8, DFF], BF16)
    wdb = P.sb("wdb", [128, 22, 1024], BF16)
    stage = P.sb("stage", [128, DFF], F32)
    gft = P.sb("gft", [128, 1024], F32)
    glt = P.sb("glt", [128, 1024], F32)
    idb = P.sb("idb", [128, 128], BF16)
    P.dma("sp", idb[:], idn, writes=["idb"])
    P.dma("sp", gft[:], gf.partition_broadcast(128), writes=["gft"])
    P.dma("sp", glt[:], gl.partition_broadcast(128), writes=["glt"])
    i = 0
    def load_w(dst, src_rows, width):
        nonlocal i
        P.dma("sp" if i % 2 == 0 else "pool", stage[:, 0:width], src_rows, writes=["stage"],
              semkey="stage_sp" if i % 2 == 0 else "stage_pool")
        P.cp("dve" if i % 2 == 0 else "pool", dst, stage[:, 0:width], reads=["stage"], writes=["wts"])
        i += 1
    for c in range(8):
        load_w(wob[:, c, :], wo[c * 128:(c + 1) * 128, :], 1024)
    for c in range(8):
        load_w(wgb[:, c, :], wg[c * 128:(c + 1) * 128, :], DFF)
    for c in range(8):
        load_w(wub[:, c, :], wu[c * 128:(c + 1) * 128, :], DFF)
    for f in range(22):
        load_w(wdb[:, f, :], wd[f * 128:(f + 1) * 128, :], 1024)
    actT = stage[:].bitcast(BF16).rearrange("p (f t) -> p f t", t=256)

    ST = 256
    yTb = [P.sb("yTb%d" % i, [128, 8, ST], BF16) for i in range(2)]
    h = [P.sb("h%d" % i, [128, 1024], F32) for i in range(2)]
    hn = [P.sb("hn%d" % i, [128, 1024], BF16) for i in range(2)]
    hnT = P.sb("hnT", [128, 8, ST], BF16)
    sg = [P.sb("sg%d" % i, [128, ST], F32) for i in range(2)]
    junk = P.sb("junk", [128, 1024], BF16)
    ss = [P.sb("ss%d" % i, [128, 4], F32) for i in range(2)]
    pA = [P.ps("pA%d" % i, [128, 512], F32) for i in range(2)]
    pG = [P.ps("pG%d" % i, [128, 512], F32) for i in range(2)]
    pU = [P.ps("pU%d" % i, [128, 512], F32) for i in range(2)]
    pT = P.ps("pT", [128, 8, 128], BF16)
    ca = 0
    cg = 0

    def rstd(hs, b, which):
        sk = "ss%d" % b
        P.memset("pool", ss[b][:], 0.0, writes=[sk])
        P.act(junk[:], hs, AF.Square, reads=["h%d" % b, sk], writes=["junk", sk], accum_out=ss[b][:, 0:1])
        P.act(ss[b][:, 1:2], ss[b][:, 0:1], AF.Sqrt, reads=[sk], writes=[sk], bias=1e-6, scale=1.0 / 1024)
        P.recip(ss[b][:, 2:3], ss[b][:, 1:2], reads=[sk], writes=[sk])

    for T in range(NTOK // ST):
        sl = T % 2
        P.dma("sp", yTb[sl][:], yT[:, T * ST:(T + 1) * ST].rearrange("(c p) t -> p c t", p=128),
              writes=["yTb%d" % sl])
        for s in range(2):
            tok0 = T * ST + s * 128
            P.dma("pool", h[s][:], x[tok0:tok0 + 128, :], writes=["h%d" % s])
            for n in range(2):
                bk = ca % 2
                ca += 1
                for c in range(8):
                    P.mm(pA[bk][:], yTb[sl][:, c, s * 128:(s + 1) * 128], wob[:, c, n * 512:(n + 1) * 512],
                         start=(c == 0), stop=(c == 7), reads=["yTb%d" % sl, "wts"], writes=["pA%d" % bk])
                P.tt("dve", h[s][:, n * 512:(n + 1) * 512], h[s][:, n * 512:(n + 1) * 512], pA[bk][:], ALU.add,
                     reads=["pA%d" % bk, "h%d" % s], writes=["h%d" % s])
            rstd(h[s][:], s, 0)
            P.stt("dve", hn[s][:], h[s][:], ss[s][:, 2:3], gft[:], ALU.mult, ALU.mult,
                  reads=["h%d" % s, "ss%d" % s, "gft"], writes=["hn%d" % s])
            for c in range(8):
                P.tr(pT[:, c, :], hn[s][:, c * 128:(c + 1) * 128], idb[:], reads=["hn%d" % s, "idb"], writes=["pT"])
            P.cp("act", hnT[:, :, s * 128:(s + 1) * 128], pT[:], reads=["pT"], writes=["hnT"])
        for f in range(22):
            bk = cg % 2
            cg += 1
            for c in range(8):
                P.mm(pG[bk][:, 0:ST], wgb[:, c, f * 128:(f + 1) * 128], hnT[:, c, :], start=(c == 0), stop=(c == 7),
                     reads=["wts", "hnT"], writes=["pG%d" % bk])
            for c in range(8):
                P.mm(pU[bk][:, 0:ST], wub[:, c, f * 128:(f + 1) * 128], hnT[:, c, :], start=(c == 0), stop=(c == 7),
                     reads=["wts", "hnT"], writes=["pU%d" % bk])
            P.act(sg[bk][:], pG[bk][:, 0:ST], AF.Silu, reads=["pG%d" % bk], writes=["sg%d" % bk])
            P.tt("dve", actT[:, f, :], sg[bk][:], pU[bk][:, 0:ST], ALU.mult,
                 reads=["sg%d" % bk, "pU%d" % bk], writes=["stage"])
        for s in range(2):
            tok0 = T * ST + s * 128
            for n in range(2):
                bk = ca % 2
                ca += 1
                for f in range(22):
                    P.mm(pA[bk][:], actT[:, f, s * 128:(s + 1) * 128], wdb[:, f, n * 512:(n + 1) * 512],
                         start=(f == 0), stop=(f == 21), reads=["stage", "wts"], writes=["pA%d" % bk])
                P.tt("dve", h[s][:, n * 512:(n + 1) * 512], h[s][:, n * 512:(n + 1) * 512], pA[bk][:], ALU.add,
                     reads=["pA%d" % bk, "h%d" % s], writes=["h%d" % s])
            rstd(h[s][:], s, 1)
            P.stt("dve", h[s][:], h[s][:], ss[s][:, 2:3], glt[:], ALU.mult, ALU.mult,
                  reads=["h%d" % s, "ss%d" % s, "glt"], writes=["h%d" % s])
            P.dma("sp", out[tok0:tok0 + 128, :], h[s][:], reads=["h%d" % s], writes=["out"], semkey="oh%d" % s)
    P.build()
    return nc
SEQ = 16384
RT = 256
NCH = 4


def build_l2(seq=SEQ):
    nc = bass.Bass("TRN2", target_bir_lowering=False)
    cT = nc.dram_tensor("cT", [6, 128, seq + 1], F32, kind="ExternalInput").ap()
    mu_d = nc.dram_tensor("mu", [128, 6], F32, kind="ExternalInput").ap()
    vec_d = nc.dram_tensor("vec", [128, 8], F32, kind="ExternalInput").ap()
    wa2_d = nc.dram_tensor("wa2", [128, 128], F32, kind="ExternalInput").ap()
    g2a_d = nc.dram_tensor("g2a", [128, 128], F32, kind="ExternalInput").ap()
    g2b_d = nc.dram_tensor("g2b", [32, 128], F32, kind="ExternalInput").ap()
    idn_d = nc.dram_tensor("idn", [128, 128], F32, kind="ExternalInput").ap()
    blk_d = nc.dram_tensor("blk", [128, 128], F32, kind="ExternalInput").ap()
    mg_d = nc.dram_tensor("mg", [64, 512], F32, kind="ExternalInput").ap()
    ma_d = nc.dram_tensor("ma", [64, 128], F32, kind="ExternalInput").ap()
    yT = nc.dram_tensor("yT", [128, seq], BF16, kind="ExternalOutput").ap()
    P = Prog(nc)
    P.psum_keys = {"pL0", "pL1", "pTm", "pTT", "pSq", "pSe", "pA"}
    cst = {}
    for nm, d, shp in (("mu", mu_d, [128, 6]), ("vec", vec_d, [128, 8]), ("wa2", wa2_d, [128, 128]),
                       ("g2a", g2a_d, [128, 128]), ("g2b", g2b_d, [32, 128]), ("idn", idn_d, [128, 128]),
                       ("blk", blk_d, [128, 128]), ("mg", mg_d, [64, 512]), ("ma", ma_d, [64, 128])):
        cst[nm] = P.sb("c_" + nm, shp, F32)
        P.dma("sp", cst[nm][:], d, writes=[nm])
    mu, vec, wa2, g2a, g2b, idn, blk, mg, ma = (cst[k] for k in ("mu", "vec", "wa2", "g2a", "g2b", "idn", "blk", "mg", "ma"))
    vx = P.sb("vx", [128, 4], F32)
    P.ts("dve", vx[:, 0:1], vec[:, 3:4], -1.0, 1.0, ALU.mult, ALU.add, reads=["vec"], writes=["vx"])
    W0, A0, KK, KA, RK, LW, LB = (vec[:, i:i + 1] for i in range(7))

    def T(name, shape=(128, RT), dt=F32):
        return P.sb(name, list(shape), dt)
    ct = [[T("ct%d_%d" % (s, g), (128, RT + 1)) for g in range(6)] for s in range(2)]
    dsh = T("dsh")
    xs = [T("xs%d" % g) for g in range(6)]
    lat = T("lat"); sg0 = T("sg0"); sg1 = T("sg1")
    lw = T("lw"); av = T("av"); gv = T("gv")
    kk = T("kk"); kk2 = T("kk2"); nrm = T("nrm"); kkn = T("kkn"); tmp = T("tmp"); kmod = T("kmod"); bb = T("bb")
    rkm = T("rkm"); bonus = T("bonus")
    csA = T("csA"); csB = T("csB"); gam = T("gam"); ginv = T("ginv"); gprev = T("gprev")
    BK = T("BK", (128, NCH, 2, 64)); AR = T("AR", (128, NCH, 2, 64))
    bh = T("bh"); kh = T("kh"); vv = xs[2]
    r1 = T("r1", (64, RT)); g1 = T("g1", (64, RT))
    tm = T("tm", (64, NCH, 4, 128))
    Gm = T("Gm", (64, NCH, 2, 2, 128))
    Pw = [T("Pw%d" % i, (64, 8, 2, 64)) for i in range(2)]
    IPw = T("IPw", (64, 8, 64))
    TTb = [T("TT%d" % i, (64, 8, 64)) for i in range(2)]
    Zv = T("Zv", (64, 8, 64)); UV = T("UV", (64, 8, 64)); WT = T("WT", (64, 8, 64))
    Yv = T("Yv", (64, 8, 64)); KV = T("KV", (64, 8, 64))
    S0 = T("S0", (64, 2, 64)); U = T("U", (64, 2, 64)); stmp = T("stmp", (64, 2, 64))
    Yt = T("Yt", (64, NCH, 128)); ysq = T("ysq", (64, NCH, 128)); yn = T("yn", (64, NCH, 128))
    st = T("st", (64, 8, 6))
    yo = [T("yo%d" % i, (128, RT)) for i in range(1)]
    yob = [T("yob%d" % i, (128, RT), BF16) for i in range(2)]
    pL0 = P.ps("pL0", [128, 512], F32); pL1 = P.ps("pL1", [128, 512], F32)
    pTm = P.ps("pTm", [64, 4, 128], F32)
    pSq = P.ps("pSq", [64, 8, 2, 64], F32)
    pTTf = P.ps("pTT", [64, 512], F32)
    pTT = pTTf[:].rearrange("p (q t) -> p q t", t=64)
    pGr = pTTf[:].rearrange("p (q t) -> p q t", t=128)
    pSe = P.ps("pSe", [64, 512], F32)
    pA = P.ps("pA", [64, 8, 64], F32)
    P.memset("dve", S0[:], 0.0, writes=["S0"])

    ntile = seq // RT
    for Ti in range(ntile):
        s = Ti % 2
        t0 = Ti * RT
        for g in range(6):
            rows = 128 if g < 5 else 32
            P.dma("sp" if g % 2 == 0 else "pool", ct[s][g][0:rows, :], cT[g, 0:rows, t0:t0 + RT + 1],
                  writes=["ct%d_%d" % (s, g)])
        for g in range(6):
            rows = 128 if g < 5 else 32
            k = "ct%d_%d" % (s, g)
            P.tt("pool", dsh[0:rows, :], ct[s][g][0:rows, 0:RT], ct[s][g][0:rows, 1:RT + 1], ALU.subtract,
                 reads=[k], writes=["dsh"])
            P.stt("dve", xs[g][0:rows, :], dsh[0:rows, :], mu[0:rows, g:g + 1], ct[s][g][0:rows, 1:RT + 1],
                  ALU.mult, ALU.add, reads=["dsh", k, "mu"], writes=["xs%d" % g])
        rr, kx = xs[0], xs[1]
        P.act(lat[0:64, :], xs[3][0:64, :], AF.Tanh, reads=["xs3"], writes=["lat"])
        P.act(sg0[:], xs[4][:], AF.Sigmoid, reads=["xs4"], writes=["sg0"])
        P.act(sg1[0:32, :], xs[5][0:32, :], AF.Sigmoid, reads=["xs5"], writes=["sg1"])
        P.mm(pL0[:, 0:RT], wa2[0:64, :], lat[0:64, :], reads=["wa2", "lat"], writes=["pL0"])
        P.mm(pL0[:, RT:2 * RT], wa2[64:128, :], xs[3][64:128, :], reads=["wa2", "xs3", "pL0"], writes=["pL0"])
        P.mm(pL1[:, 0:RT], g2a[:], sg0[:], start=True, stop=False, reads=["g2a", "sg0"], writes=["pL1"])
        P.mm(pL1[:, 0:RT], g2b[0:32, :], sg1[0:32, :], start=False, stop=True, reads=["g2b", "sg1"], writes=["pL1"])
        P.act(lw[:], pL0[:, 0:RT], AF.Sigmoid, reads=["pL0", "vec"], writes=["lw"], bias=W0)
        P.act(av[:], pL0[:, RT:2 * RT], AF.Sigmoid, reads=["pL0", "vec"], writes=["av"], bias=A0)
        P.cp("act", gv[:], pL1[:, 0:RT], reads=["pL1"], writes=["gv"])
        P.ts("pool", lw[:], lw[:], -0.6065306597126334, None, ALU.mult, reads=["lw"], writes=["lw"])
        P.ts("pool", kk[:], kx[:], KK, None, ALU.mult, reads=["xs1", "vec"], writes=["kk"])
        P.tt("pool", kk2[:], kk[:], kk[:], ALU.mult, reads=["kk"], writes=["kk2"])
        P.mm(pL0[:, 0:RT], blk[:], kk2[:], reads=["blk", "kk2"], writes=["pL0"])
        P.act(nrm[:], pL0[:, 0:RT], AF.Sqrt, reads=["pL0"], writes=["nrm"])
        P.ts("dve", nrm[:], nrm[:], 1e-12, None, ALU.max, reads=["nrm"], writes=["nrm"])
        P.recip(nrm[:], nrm[:], reads=["nrm"], writes=["nrm"])
        P.tt("dve", kkn[:], kk[:], nrm[:], ALU.mult, reads=["kk", "nrm"], writes=["kkn"])
        P.ts("dve", tmp[:], av[:], KA, vx[:, 0:1], ALU.mult, ALU.add, reads=["av", "vec", "vx"], writes=["tmp"])
        P.tt("dve", kmod[:], kx[:], tmp[:], ALU.mult, reads=["xs1", "tmp"], writes=["kmod"])
        P.tt("pool", bb[:], kkn[:], av[:], ALU.mult, reads=["kkn", "av"], writes=["bb"])
        P.stt("dve", rkm[:], rr[:], RK, kmod[:], ALU.mult, ALU.mult, reads=["xs0", "vec", "kmod"], writes=["rkm"])
        P.mm(pL0[:, RT:2 * RT], blk[:], rkm[:], reads=["blk", "rkm"], writes=["pL0"])
        P.tt("dve", bonus[:], pL0[:, RT:2 * RT], vv[:], ALU.mult, reads=["pL0", "xs2"], writes=["bonus"])
        src, dst, ks, kd = lw, csA, "lw", "csA"
        for d in (1, 2, 4, 8, 16, 32):
            s3 = src[:].rearrange("p (c t) -> p c t", t=64)
            d3 = dst[:].rearrange("p (c t) -> p c t", t=64)
            P.cp("pool", d3[:, :, 0:d], s3[:, :, 0:d], reads=[ks], writes=[kd])
            P.tt("dve", d3[:, :, d:64], s3[:, :, d:64], s3[:, :, 0:64 - d], ALU.add, reads=[ks], writes=[kd])
            if dst is csA:
                src, dst, ks, kd = csA, csB, "csA", "csB"
            else:
                src, dst, ks, kd = csB, csA, "csB", "csA"
        cs, kcs = src, ks
        P.act(gam[:], cs[:], AF.Exp, reads=[kcs], writes=["gam"])
        P.act(ginv[:], cs[:], AF.Exp, reads=[kcs], writes=["ginv"], scale=-1.0)
        P.tt("pool", tmp[:], cs[:], lw[:], ALU.subtract, reads=[kcs, "lw", "kmod"], writes=["tmp"])
        P.act(gprev[:], tmp[:], AF.Exp, reads=["tmp"], writes=["gprev"])
        v4 = lambda t: t[:].rearrange("p (c t) -> p c t", t=64)
        P.tt("dve", BK[:, :, 0, :], v4(bb), v4(ginv), ALU.mult, reads=["bb", "ginv"], writes=["BK"])
        P.tt("dve", BK[:, :, 1, :], v4(kmod), v4(ginv), ALU.mult, reads=["kmod", "ginv"], writes=["BK"])
        P.stt("dve", AR[:, :, 0, :], v4(kkn), -1.0, v4(gprev), ALU.mult, ALU.mult, reads=["kkn", "gprev"], writes=["AR"])
        P.tt("pool", AR[:, :, 1, :], v4(rr), v4(gam), ALU.mult, reads=["xs0", "gam"], writes=["AR"])
        gC = v4(gam)[:, :, 63:64].to_broadcast([128, NCH, 64])
        P.tt("dve", v4(bh), BK[:, :, 0, :], gC, ALU.mult, reads=["BK", "gam"], writes=["bh"])
        P.tt("pool", v4(kh), BK[:, :, 1, :], gC, ALU.mult, reads=["BK", "gam"], writes=["kh"])
        P.dma("sp", r1[:].rearrange("p (c t) -> p c t", t=64), AR[64:128, :, 1, :], reads=["AR"], writes=["r1"])
        P.dma("sp", g1[:], gam[64:128, :], reads=["gam"], writes=["g1"])
        for c in range(NCH):
            cs_ = slice(c * 64, (c + 1) * 64)
            for i, (src_t, key) in enumerate(((bh[:, cs_], "bh"), (kh[:, cs_], "kh"), (AR[:, c, 0, :], "AR"), (vv[:, cs_], "xs2"))):
                P.mm(pTm[:, i, :], src_t, idn[:], reads=[key, "idn"], writes=["pTm"])
            P.cp("act", tm[:, c, :, :], pTm[:], reads=["pTm"], writes=["tm"])
            for h in range(2):
                hp = slice(h * 64, (h + 1) * 64)
                arh = AR[hp, c, :, :].rearrange("p a t -> p (a t)")
                P.mm(pGr[:, 2 * h, :], BK[hp, c, 0, :], arh, reads=["BK", "AR", "pTT"], writes=["pTT"])
                P.mm(pGr[:, 2 * h + 1, :], BK[hp, c, 1, :], arh, reads=["BK", "AR", "pTT"], writes=["pTT"])
                P.mm(pA[:, 2 * c + h, :], AR[hp, c, 0, :], BK[hp, c, 0, :], reads=["BK", "AR", "pA"], writes=["pA"])
            P.tt("dve", Gm[:, c, :, :, :].rearrange("p h w t -> p (h w) t"), pGr, mg[:].rearrange("p (a t) -> p a t", t=128),
                 ALU.mult, reads=["pTT", "mg"], writes=["Gm"])
        ma8 = ma[:, 0:64].unsqueeze(1).to_broadcast([64, 8, 64])
        id8 = idn[0:64, 0:64].unsqueeze(1).to_broadcast([64, 8, 64])
        P.tt("dve", Pw[0][:, :, 0, :], pA[:], ma8, ALU.mult, reads=["pA", "ma"], writes=["Pw0"])
        labT = Gm[:, :, :, 0, 0:64].rearrange("p c h t -> p (c h) t")
        P.cp("pool", Pw[0][:, :, 1, :], labT, reads=["Gm"], writes=["Pw0"])
        P.tt("dve", TTb[0][:], labT, id8, ALU.add, reads=["Gm", "idn"], writes=["TT0"])
        cur = 0
        for lvl in range(5):
            nxt = 1 - cur
            kc, kn = "Pw%d" % cur, "Pw%d" % nxt
            for q in range(8):
                P.mm(pSq[:, q, 0, :], Pw[cur][:, q, 1, :], Pw[cur][:, q, 0, :], reads=[kc], writes=["pSq"])
                if lvl < 4:
                    P.mm(pSq[:, q, 1, :], Pw[cur][:, q, 0, :], Pw[cur][:, q, 1, :], reads=[kc], writes=["pSq"])
            id4 = idn[0:64, 0:64].unsqueeze(1).to_broadcast([64, 4, 64])
            for hf in range(2):
                qs = slice(4 * hf, 4 * hf + 4)
                P.tt("dve", IPw[:, qs, :], pSq[:, qs, 0, :], id4, ALU.add, reads=["pSq", "idn"], writes=["IPw"])
                if lvl < 4:
                    P.cp("act", Pw[nxt][:, qs, :, :], pSq[:, qs, :, :], reads=["pSq"], writes=[kn])
            for q in range(8):
                P.mm(pTT[:, q, :], IPw[:, q, :], TTb[cur][:, q, :], reads=["IPw", "TT%d" % cur], writes=["pTT"])
            P.cp("act", TTb[nxt][:], pTT, reads=["pTT"], writes=["TT%d" % nxt])
            cur = nxt
        TT = TTb[cur]; kTT = "TT%d" % cur
        def q_of(c, h):
            return 2 * c + h
        for c in range(NCH):
            for h in range(2):
                q = q_of(c, h); hs = slice(h * 64, (h + 1) * 64)
                P.mm(pA[:, q, :], Gm[:, c, h, 1, 0:64], tm[:, c, 3, hs], reads=["Gm", "tm", "Pw0"], writes=["pA"])
        P.cp("dve", Zv[:], pA[:], reads=["pA"], writes=["Zv"])
        for c in range(NCH):
            for h in range(2):
                q = q_of(c, h); hs = slice(h * 64, (h + 1) * 64)
                P.mm(pTT[:, q, :], TT[:, q, :], Zv[:, q, :], reads=[kTT, "Zv"], writes=["pTT"])
                P.mm(pSq[:, q, 0, :], tm[:, c, 2, hs], TT[:, q, :], reads=["tm", kTT], writes=["pSq"])
                P.mm(pSq[:, q, 1, :], Gm[:, c, h, 1, 64:128], tm[:, c, 3, hs], reads=["Gm", "tm"], writes=["pSq"])
                P.mm(pA[:, q, :], tm[:, c, 1, hs], tm[:, c, 3, hs], reads=["tm", "Zv"], writes=["pA"])
        P.cp("act", UV[:], pTT, reads=["pTT"], writes=["UV"])
        for hf in range(2):
            qs = slice(4 * hf, 4 * hf + 4)
            P.cp("dve", WT[:, qs, :], pSq[:, qs, 0, :], reads=["pSq"], writes=["WT"])
            P.cp("act", Yv[:, qs, :], pSq[:, qs, 1, :], reads=["pSq"], writes=["Yv"])
        P.cp("dve", KV[:], pA[:], reads=["pA"], writes=["KV"])
        pU = pSe[:, 0:128].rearrange("p (h v) -> p h v", v=64)
        pY = pSe[:, 128:256].rearrange("p (h v) -> p h v", v=64)
        pS = pSe[:, 256:384].rearrange("p (h v) -> p h v", v=64)
        for c in range(NCH):
            for h in range(2):
                P.mm(pU[:, h, :], WT[:, q_of(c, h), :], S0[:, h, :], reads=["WT", "S0"], writes=["pSe"])
            P.tt("dve", U[:], pU, UV[:, 2 * c:2 * c + 2, :], ALU.add, reads=["pSe", "UV"], writes=["U"])
            for h in range(2):
                hs = slice(h * 64, (h + 1) * 64)
                rk_ = AR[0:64, c, 1, :] if h == 0 else r1[:, c * 64:(c + 1) * 64]
                P.mm(pY[:, h, :], rk_, S0[:, h, :], start=True, stop=False, reads=["AR", "r1", "S0"], writes=["pSe"])
                P.mm(pY[:, h, :], Gm[:, c, h, 0, 64:128], U[:, h, :], start=False, stop=True, reads=["Gm", "U"], writes=["pSe"])
                P.mm(pS[:, h, :], tm[:, c, 0, hs], U[:, h, :], reads=["tm", "U"], writes=["pSe"])
            P.tt("dve", Yt[:, c, :].rearrange("p (h v) -> p h v", v=64), pY, Yv[:, 2 * c:2 * c + 2, :], ALU.add,
                 reads=["pSe", "Yv"], writes=["Yt"])
            P.tt("dve", stmp[:], pS, KV[:, 2 * c:2 * c + 2, :], ALU.add, reads=["pSe", "KV"], writes=["stmp"])
            for h in range(2):
                gcol = gam[0:64, c * 64 + 63:c * 64 + 64] if h == 0 else g1[:, c * 64 + 63:c * 64 + 64]
                P.stt("dve", S0[:, h, :], S0[:, h, :], gcol, stmp[:, h, :], ALU.mult, ALU.add,
                      reads=["S0", "gam", "g1", "stmp"], writes=["S0"])
        Y8 = Yt[:].rearrange("p c (h v) -> p (c h) v", v=64)
        P.op("dve", lambda e: e.tensor_reduce(st[:, :, 0], Y8, AX.X, ALU.add), ["Yt"], ["st"])
        P.tt("pool", ysq[:], Yt[:], Yt[:], ALU.mult, reads=["Yt"], writes=["ysq"])
        P.op("dve", lambda e: e.tensor_reduce(st[:, :, 1], ysq[:].rearrange("p c (h v) -> p (c h) v", v=64), AX.X, ALU.add), ["ysq"], ["st"])
        P.ts("dve", st[:, :, 2], st[:, :, 0], 1.0 / 64, None, ALU.mult, reads=["st"], writes=["st"])
        P.tt("dve", st[:, :, 3], st[:, :, 2], st[:, :, 2], ALU.mult, reads=["st"], writes=["st"])
        P.stt("dve", st[:, :, 4], st[:, :, 1], 1.0 / 64, st[:, :, 3], ALU.mult, ALU.subtract, reads=["st"], writes=["st"])
        P.act(st[:, :, 5], st[:, :, 4], AF.Sqrt, reads=["st"], writes=["st"], bias=64e-5)
        P.recip(st[:, :, 5], st[:, :, 5], reads=["st"], writes=["st"])
        yn8 = yn[:].rearrange("p c (h v) -> p (c h) v", v=64)
        P.tt("dve", yn8, Y8, st[:, :, 2:3].to_broadcast([64, 8, 64]), ALU.subtract, reads=["Yt", "st"], writes=["yn"])
        P.tt("dve", yn8, yn8, st[:, :, 5:6].to_broadcast([64, 8, 64]), ALU.mult, reads=["yn", "st"], writes=["yn"])
        for c in range(NCH):
            P.mm(pL1[:, RT + c * 64:RT + (c + 1) * 64], yn[:, c, :], idn[0:64, 0:64], reads=["yn", "idn"], writes=["pL1"])
        P.ts("dve", yo[0][:], pL1[:, RT:2 * RT], LW, LB, ALU.mult, ALU.add, reads=["pL1", "vec"], writes=["yo"])
        P.tt("pool", yo[0][:], yo[0][:], bonus[:], ALU.add, reads=["yo", "bonus"], writes=["yo"])
        P.tt("dve", yob[s][:], yo[0][:], gv[:], ALU.mult, reads=["yo", "gv"], writes=["yob%d" % s])
        P.dma("sp", yT[:, t0:t0 + RT], yob[s][:], reads=["yob%d" % s], writes=["yT"], semkey="yob%d" % s)
    P.build()
    return nc
PI = 3.141592653589793
TWO_PI = 6.283185307179586
RC1 = 6.28125
RC2 = TWO_PI - RC1


def build_l3(seq=16384):
    nc = bass.Bass("TRN2", target_bir_lowering=False)
    NKT = seq // 128
    NQB = NKT // 2
    NCMP = seq // 16 - 1
    NCT = (NCMP + 127) // 128
    NBLK = seq // 64
    di = lambda n, shp, dt=F32: nc.dram_tensor(n, shp, dt, kind="ExternalInput").ap()
    kT_d = di("kT", [128, seq]); kTs_d = di("kTs", [128, seq])
    v_d = di("v", [seq, 128]); vs_d = di("vs", [seq, 128])
    cmpT_d = di("cmpT", [128, seq])
    posr_d = di("posr", [1, seq], I32)
    post_d = di("post", [128, NKT], I32)
    posc_d = di("posc", [128, NCT], I32)
    qT_d = di("qT", [NQB, 128, 512]); qTs_d = di("qTs", [NQB, 128, 512])
    posq_d = di("posq", [NQB, 1, 512], I32)
    gate_d = di("gate", [NQB, 128, 12])
    w1_d = di("w1", [128, 32, 256]); peT_d = di("peT", [128, 32]); b1_d = di("b1", [128, 4])
    w2_d = di("w2", [128, 4, 64])
    frow_d = di("frow", [1, 128]); srow_d = di("srow", [1, 128]); fpp_d = di("fpp", [128, 2])
    idb_d = di("idb", [128, 128], BF16); idf_d = di("idf", [128, 128])
    F_d = di("Fx", [128, 8192], BF16)
    MC_d = di("MC", [128, 9, 128], BF16)
    mAB_d = di("mAB", [128, 2, 128], BF16)
    wm_d = di("wm", [128, 6, 128], BF16)
    m32_d = di("m32", [128, 32])
    pat_d = di("pat", [128, 1024])
    yo_d = nc.dram_tensor("yo", [NQB, 128, 256], BF16, kind="ExternalOutput").ap()
    P = Prog(nc)
    P.psum_keys = {"b0", "b1", "b2", "b3", "b4", "b56"}

    def T(name, shape, dt=F32):
        return P.sb("s_" + name, list(shape), dt)

    def cload(name, d, shape, dt=F32, bcast=False):
        t = T("c_" + name, shape, dt)
        P.dma("sp", t[:], d.partition_broadcast(128) if bcast else d, writes=[name])
        return t
    frow = cload("frow", frow_d, [128, 128], bcast=True); srow = cload("srow", srow_d, [128, 128], bcast=True)
    fpp = cload("fpp", fpp_d, [128, 2]); idb = cload("idb", idb_d, [128, 128], BF16); idf = cload("idf", idf_d, [128, 128])
    Fx = cload("Fx", F_d, [128, 8192], BF16); MC = cload("MC", MC_d, [128, 9, 128], BF16)
    mAB = cload("mAB", mAB_d, [128, 2, 128], BF16); wm = cload("wm", wm_d, [128, 6, 128], BF16)
    m32 = cload("m32", m32_d, [128, 32]); pat = cload("pat", pat_d, [128, 1024])
    post_i = cload("posti", post_d, [128, NKT], I32); posc_i = cload("posci", posc_d, [128, NCT], I32)
    peT = cload("peT", peT_d, [128, 32]); b1 = cload("b1", b1_d, [128, 4]); w2f = cload("w2f", w2_d, [128, 4, 64])
    post = T("post", [128, NKT]); posc = T("posc", [128, NCT])
    P.cp("dve", post[:], post_i[:], reads=["posti"], writes=["post"])
    P.cp("dve", posc[:], posc_i[:], reads=["posci"], writes=["posc"])
    w2b = T("w2b", [128, 4, 64], BF16)
    P.cp("dve", w2b[:], w2f[:], reads=["w2f"], writes=["w2b"])
    peb = T("peb", [128, 32], BF16)
    P.cp("dve", peb[:], peT[:], reads=["peT"], writes=["peb"])
    hpi = T("hpi", [128, 1])
    P.memset("pool", hpi[:], PI / 2, writes=["hpi"])

    KT = T("KT", [128, seq], BF16)
    VA = T("VA", [128, NKT, 2, 65], BF16)
    CT = T("CT", [128, seq], BF16)
    W1 = T("W1", [128, 32, 256], BF16)
    KcT = T("KcT", [64, NCT * 128], BF16)
    VcA = T("VcA", [128, NCT, 65], BF16)
    hidS = T("hidS", [128, 2, 2, NCT * 128], BF16)
    bank = [P.ps("bank%d" % i, [128, 512], F32) for i in range(5)]
    b56 = P.ps("b56", [128, 1024], F32)
    stg = [T("stg%d" % i, [128, 2048]) for i in range(2)]
    ang = T("ang", [128, 512]); tC = T("tC", [128, 512]); tS = T("tS", [128, 512])
    t1 = T("t1", [128, 512]); t2 = T("t2", [128, 512])
    posb_i = T("posb_i", [128, 512], I32); posb = T("posb", [128, 512])

    def cos_of(dst, a, n, shift, kd):
        src = a
        if shift != 0.0:
            P.ts("dve", t2[:, 0:n], a, shift, None, ALU.add, reads=["ang"], writes=["t2"])
            src = t2[:, 0:n]
        P.ts("dve", t1[:, 0:n], src, 1.0 / TWO_PI, 8388608.0, ALU.mult, ALU.add, reads=["ang", "t2"], writes=["t1"])
        P.ts("dve", t1[:, 0:n], t1[:, 0:n], -8388608.0, None, ALU.add, reads=["t1"], writes=["t1"])
        P.stt("dve", t2[:, 0:n], t1[:, 0:n], -RC1, src, ALU.mult, ALU.add, reads=["t1", "t2", "ang"], writes=["t2"])
        P.stt("dve", t2[:, 0:n], t1[:, 0:n], -RC2, t2[:, 0:n], ALU.mult, ALU.add, reads=["t1", "t2"], writes=["t2"])
        P.stt("dve", t1[:, 0:n], t2[:, 0:n], -1.0, t2[:, 0:n], ALU.mult, ALU.max, reads=["t2"], writes=["t1"])
        P.ts("dve", t1[:, 0:n], t1[:, 0:n], PI, None, ALU.min, reads=["t1"], writes=["t1"])
        P.act(dst, t1[:, 0:n], AF.Sin, reads=["t1", "hpi"], writes=[kd], bias=hpi[:, 0:1], scale=-1.0)

    def tables(n, sign_ap, sign_is_row):
        cos_of(tC[:, 0:n], ang[:, 0:n], n, 0.0, "tC")
        cos_of(tS[:, 0:n], ang[:, 0:n], n, -PI / 2, "tS")
        if sign_is_row:
            P.tt("dve", tS[:, 0:n], tS[:, 0:n], sign_ap, ALU.mult, reads=["tS", "srow"], writes=["tS"])
        else:
            P.ts("dve", tS[:, 0:n], tS[:, 0:n], sign_ap, None, ALU.mult, reads=["tS", "fpp"], writes=["tS"])

    def rope_fm(dst, x_ap, xs_ap, pos_row_d, n, kx, kd):
        P.dma("pool", posb_i[:, 0:n], pos_row_d.partition_broadcast(128), writes=["posb_i"])
        P.cp("dve", posb[:, 0:n], posb_i[:, 0:n], reads=["posb_i"], writes=["posb"])
        P.ts("dve", ang[:, 0:n], posb[:, 0:n], fpp[:, 0:1], None, ALU.mult, reads=["posb", "fpp"], writes=["ang"])
        tables(n, fpp[:, 1:2], False)
        P.tt("dve", tC[:, 0:n], tC[:, 0:n], x_ap, ALU.mult, reads=["tC", kx], writes=["tC"])
        P.tt("pool", tS[:, 0:n], tS[:, 0:n], xs_ap, ALU.mult, reads=["tS", kx], writes=["tS"])
        P.tt("dve", dst, tC[:, 0:n], tS[:, 0:n], ALU.add, reads=["tC", "tS"], writes=[kd])

    for i in range(4):
        P.dma("sp", stg[i % 2][:].rearrange("p (l h) -> p l h", h=256), w1_d[:, i * 8:(i + 1) * 8, :], writes=["stg%d" % (i % 2)])
        P.cp("dve" if i % 2 == 0 else "pool", W1[:, i * 8:(i + 1) * 8, :], stg[i % 2][:].rearrange("p (l h) -> p l h", h=256),
             reads=["stg%d" % (i % 2)], writes=["W1"])
    for t in range(seq // 512):
        c0 = t * 512
        P.dma("sp", stg[0][:, 0:512], kT_d[:, c0:c0 + 512], writes=["stg0"])
        P.dma("sp", stg[0][:, 512:1024], kTs_d[:, c0:c0 + 512], writes=["stg0"])
        P.dma("sp", stg[0][:, 1024:1536], cmpT_d[:, c0:c0 + 512], writes=["stg0"])
        rope_fm(KT[:, c0:c0 + 512], stg[0][:, 0:512], stg[0][:, 512:1024], posr_d[:, c0:c0 + 512], 512, "stg0", "KT")
        P.cp("pool", CT[:, c0:c0 + 512], stg[0][:, 1024:1536], reads=["stg0"], writes=["CT"])
    P.memset("pool", VA[:], 1.0, writes=["VA"])
    for kt in range(NKT):
        P.dma("sp", stg[1][:, 0:128], v_d[kt * 128:(kt + 1) * 128, :], writes=["stg1"])
        P.dma("sp", stg[1][:, 128:256], vs_d[kt * 128:(kt + 1) * 128, :], writes=["stg1"])
        P.ts("dve", ang[:, 0:128], frow[:], post[:, kt:kt + 1], None, ALU.mult, reads=["frow", "post"], writes=["ang"])
        tables(128, srow[:], True)
        P.tt("dve", tC[:, 0:128], tC[:, 0:128], stg[1][:, 0:128], ALU.mult, reads=["tC", "stg1"], writes=["tC"])
        P.tt("pool", tS[:, 0:128], tS[:, 0:128], stg[1][:, 128:256], ALU.mult, reads=["tS", "stg1"], writes=["tS"])
        P.tt("dve", VA[:, kt, :, 0:64], tC[:, 0:128].rearrange("p (a d) -> p a d", d=64),
             tS[:, 0:128].rearrange("p (a d) -> p a d", d=64), ALU.add, reads=["tC", "tS"], writes=["VA"])
    P.memset("pool", hidS[:], 0.0, writes=["hidS"])
    P.memset("pool", KcT[:], 0.0, writes=["KcT"])
    P.memset("pool", VcA[:], 1.0, writes=["VcA"])
    bias_c = T("bias_c", [128, 4])
    for kv in range(2):
        rows = slice(kv * 64, kv * 64 + 64)
        key = "b%d" % kv
        for half in range(2):
            for l in range(32):
                P.mm(bank[kv][:, half:half + 1], W1[rows, l, half * 128:(half + 1) * 128], peb[rows, l:l + 1],
                     start=(l == 0), stop=(l == 31), reads=["W1", "peb"], writes=[key])
        P.tt("dve", bias_c[:, kv * 2:kv * 2 + 2], bank[kv][:, 0:2], b1[:, kv * 2:kv * 2 + 2], ALU.add,
             reads=[key, "b1"], writes=["bias_c"])
    for n0 in range(0, NCMP, 512):
        nn = min(512, NCMP - n0)
        for kv in range(2):
            rows = slice(kv * 64, kv * 64 + 64)
            for half in range(2):
                bk = bank[2 * kv + half]
                key = "b%d" % (2 * kv + half)
                for l in range(32):
                    rhs = CT[rows, 16 * n0 + l:16 * n0 + l + 16 * (nn - 1) + 1:16]
                    P.mm(bk[:, 0:nn], W1[rows, l, half * 128:(half + 1) * 128], rhs, start=(l == 0), stop=(l == 31),
                         reads=["W1", "CT"], writes=[key])
                P.act(hidS[:, kv, half, n0:n0 + nn], bk[:, 0:nn], AF.Silu, reads=[key, "bias_c"], writes=["hidS"],
                      bias=bias_c[:, kv * 2 + half:kv * 2 + half + 1])
    kcf = T("kcf", [128, 64]); kcs = T("kcs", [128, 64]); kcb = T("kcb", [128, 64], BF16)
    pTb = bank[2][:].bitcast(BF16)
    for ct in range(NCT):
        for kv in range(2):
            for half in range(2):
                P.mm(bank[kv][:, 0:64], hidS[:, kv, half, ct * 128:(ct + 1) * 128], w2b[:, kv * 2 + half, :],
                     start=(half == 0), stop=(half == 1), reads=["hidS", "w2b"], writes=["b%d" % kv])
        P.cp("act", VcA[:, ct, 0:64], bank[1][:, 0:64], reads=["b1"], writes=["VcA"])
        P.cp("act", kcf[:], bank[0][:, 0:64], reads=["b0"], writes=["kcf"])
        P.memset("pool", kcs[:], 0.0, writes=["kcs"])
        P.cp("pool", kcs[:, 0:8], kcf[:, 8:16], reads=["kcf"], writes=["kcs"])
        P.cp("pool", kcs[:, 8:16], kcf[:, 0:8], reads=["kcf"], writes=["kcs"])
        P.ts("dve", ang[:, 0:64], frow[:, 0:64], posc[:, ct:ct + 1], None, ALU.mult, reads=["frow", "posc"], writes=["ang"])
        tables(64, srow[:, 0:64], True)
        P.tt("dve", tC[:, 0:64], tC[:, 0:64], kcf[:], ALU.mult, reads=["tC", "kcf"], writes=["tC"])
        P.tt("pool", tS[:, 0:64], tS[:, 0:64], kcs[:], ALU.mult, reads=["tS", "kcs"], writes=["tS"])
        P.tt("dve", kcb[:], tC[:, 0:64], tS[:, 0:64], ALU.add, reads=["tC", "tS"], writes=["kcb"])
        P.tr(pTb[0:64, 0:128], kcb[:], idb[:], reads=["kcb", "idb"], writes=["b2"])
        P.cp("act", KcT[:, ct * 128:(ct + 1) * 128], pTb[0:64, 0:128], reads=["b2"], writes=["KcT"])

    qr = T("qr", [128, 512], BF16)
    gt = T("gt", [128, 12]); gs_ = T("gs_", [128, 12])
    eC = T("eC", [128, 1024]); acc = T("acc", [128, 1024]); sm = T("sm", [128, 8])
    imp = T("imp", [128, 256]); sc = T("sc", [128, 256]); scw = T("scw", [128, 256]); m8 = T("m8", [128, 16])
    selb = T("selb", [128, 256], BF16); biasT = T("biasT", [128, 2, 512], BF16)
    E = [T("E%d" % i, [128, 512], BF16) for i in range(2)]
    OT = T("OT", [65, 512]); wv = T("wv", [128, 12]); ytmp = T("ytmp", [128, 4, 64]); y = T("y", [128, 4, 64]); yb = T("yb", [128, 256], BF16)
    pS = [bank[0], bank[1]]
    pM, kM = bank[2], "b2"
    pO, kO = bank[3], "b3"
    pOt, kOt = bank[4], "b4"
    pOt3 = pOt[:, 0:260].rearrange("p (g d) -> p g d", d=65)
    stepc = [0]
    E3 = lambda t: t[:].rearrange("p (g q) -> p g q", q=128)

    def attend(steps, qrows, br):
        n = len(steps)
        for i, (kl, vap, kt, cm, mkeys) in enumerate(steps):
            sl = stepc[0] % 2
            stepc[0] += 1
            bk = "b%d" % sl
            P.mm(pS[sl][:], kl, qr[qrows, :], start=True, stop=(kt is None), reads=["KT", "KcT", "qr"], writes=[bk])
            if kt is not None:
                P.mm(pS[sl][:], Fx[:, 128 * (kt % 64):128 * (kt % 64) + 128], biasT[:, kt // 64, :], start=False, stop=True,
                     reads=["Fx", "biasT", bk], writes=[bk])
            P.act(E[sl][:], pS[sl][:], AF.Exp, reads=[bk], writes=["E%d" % sl])
            if cm is not None:
                P.tt("dve", E3(E[sl]), E3(E[sl]), cm.unsqueeze(1).to_broadcast([128, 4, 128]), ALU.mult,
                     reads=["E%d" % sl] + mkeys, writes=["E%d" % sl])
            P.mm(pO[0:65, :], vap, E[sl][:], start=(i == 0), stop=(i == n - 1), reads=["E%d" % sl, "VA", "VcA"], writes=[kO])
        P.cp("act", OT[:], pO[0:65, :], reads=[kO], writes=["OT"])
        for g in range(4):
            P.mm(pOt3[:, g, :], OT[:, g * 128:(g + 1) * 128], idf[0:65, 0:65], reads=["OT", "idf"], writes=[kOt])
        P.ts("dve", wv[:, 0:4], pOt3[:, :, 64], 1e-30, None, ALU.max, reads=[kOt], writes=["wv"])
        P.recip(wv[:, 4:8], wv[:, 0:4], reads=["wv"], writes=["wv"])
        P.tt("dve", wv[:, 8:12], wv[:, 4:8], gs_[:, br:12:3], ALU.mult, reads=["wv", "gs_"], writes=["wv"])
        P.tt("dve", ytmp[:], pOt3[:, :, 0:64], wv[:, 8:12].unsqueeze(2).to_broadcast([128, 4, 64]), ALU.mult,
             reads=[kOt, "wv"], writes=["ytmp"])
        if br == 0:
            P.cp("pool", y[:], ytmp[:], reads=["ytmp"], writes=["y"])
        else:
            P.tt("pool", y[:], y[:], ytmp[:], ALU.add, reads=["ytmp", "y"], writes=["y"])

    for iq in range(NQB):
        P.dma("sp", stg[1][:, 0:512], qT_d[iq], writes=["stg1"])
        P.dma("sp", stg[1][:, 512:1024], qTs_d[iq], writes=["stg1"])
        P.dma("sp", gt[:], gate_d[iq], writes=["gt"])
        rope_fm(stg[1][:, 1024:1536], stg[1][:, 0:512], stg[1][:, 512:1024], posq_d[iq], 512, "stg1", "stg1")
        P.ts("dve", qr[:], stg[1][:, 1024:1536], 0.125, None, ALU.mult, reads=["stg1"], writes=["qr"])
        P.act(gs_[:], gt[:], AF.Sigmoid, reads=["gt"], writes=["gs_"])
        Nc = min(16 * (iq + 1), NCMP)
        c_lo = max(0, 16 * iq - 16)
        m_lo = c_lo - (16 * iq - 16)
        P.memset("pool", acc[:], 0.0, writes=["acc"])
        for g in range(4):
            for c0 in range(0, Nc, 512):
                cw = min(512, Nc - c0)
                P.mm(b56[:, c0:c0 + cw], qr[0:64, g * 128:(g + 1) * 128], KcT[:, c0:c0 + cw], reads=["qr", "KcT"], writes=["b56"])
            P.act(eC[:, 0:Nc], b56[:, 0:Nc], AF.Exp, reads=["b56"], writes=["eC"])
            P.tt("dve", eC[:, c_lo:Nc], eC[:, c_lo:Nc], m32[:, m_lo:m_lo + Nc - c_lo], ALU.mult, reads=["eC", "m32"], writes=["eC"])
            P.op("dve", lambda e, Nc=Nc: e.tensor_reduce(sm[:, 0:1], eC[:, 0:Nc], AX.X, ALU.add), ["eC"], ["sm"])
            P.ts("dve", sm[:, 1:2], sm[:, 0:1], 1e-30, None, ALU.max, reads=["sm"], writes=["sm"])
            P.recip(sm[:, 2:3], sm[:, 1:2], reads=["sm"], writes=["sm"])
            P.stt("dve", acc[:, 0:Nc], eC[:, 0:Nc], sm[:, 2:3], acc[:, 0:Nc], ALU.mult, ALU.add, reads=["eC", "sm", "acc"], writes=["acc"])
        a4 = acc[:].rearrange("p (j r) -> p j r", r=4)
        P.op("dve", lambda e: e.tensor_reduce(imp[:], a4, AX.X, ALU.add), ["acc"], ["imp"])
        P.tt("dve", imp[:, 1:256], imp[:, 1:256], a4[:, 0:255, 3], ALU.add, reads=["imp", "acc"], writes=["imp"])
        po = 256 - 4 * iq
        P.tt("dve", sc[:, 0:NBLK], imp[:, 0:NBLK], pat[:, po:po + NBLK], ALU.mult, reads=["imp", "pat"], writes=["sc"])
        P.tt("dve", sc[:, 0:NBLK], sc[:, 0:NBLK], pat[:, 512 + po:512 + po + NBLK], ALU.add, reads=["sc", "pat"], writes=["sc"])
        P.memset("dve", sc[:, 0:1], 1e6, writes=["sc"])
        P.op("dve", lambda e: e.max(out=m8[:, 0:8], in_=sc[:, 0:NBLK]), ["sc"], ["m8"])
        P.op("dve", lambda e: e.match_replace(out=scw[:, 0:NBLK], in_to_replace=m8[:, 0:8], in_values=sc[:, 0:NBLK], imm_value=-3e38),
             ["sc", "m8"], ["scw"])
        P.op("dve", lambda e: e.max(out=m8[:, 8:16], in_=scw[:, 0:NBLK]), ["scw"], ["m8"])
        P.memset("pool", selb[:], 0.0, writes=["selb"])
        P.ts("dve", selb[:, 0:NBLK], sc[:, 0:NBLK], m8[:, 15:16], None, ALU.is_ge, reads=["sc", "m8", "selb"], writes=["selb"])
        pMb = pM[:].bitcast(BF16)
        for ch in range(2):
            P.tr(pMb[:, ch * 128:(ch + 1) * 128], selb[:, ch * 128:(ch + 1) * 128], idb[:], reads=["selb", "idb"], writes=[kM])
        for ch in range(2):
            P.ts("dve", biasT[:, ch, :].rearrange("p (g q) -> p g q", q=128),
                 pMb[:, ch * 128:(ch + 1) * 128].unsqueeze(1).to_broadcast([128, 4, 128]), -1.0, 30000.0, ALU.add, ALU.mult,
                 reads=[kM], writes=["biasT"])
        KC = (Nc - 1) // 128
        steps = []
        for ktc in range(KC + 1):
            if ktc == KC:
                m = MC[:, iq % 8, :]
            elif ktc == KC - 1 and iq % 8 == 0:
                m = MC[:, 8, :]
            else:
                m = None
            steps.append((KcT[:, ktc * 128:(ktc + 1) * 128], VcA[:, ktc, :], None, m, ["MC"]))
        attend(steps, slice(0, 64), 0)
        steps = []
        for kt in range(2 * iq + 2):
            extra = mAB[:, kt - 2 * iq, :] if kt >= 2 * iq else None
            steps.append((KT[0:64, kt * 128:(kt + 1) * 128], VA[:, kt, 0, :], kt, extra, ["mAB"]))
        attend(steps, slice(0, 64), 1)
        steps = []
        for i in range(6):
            kt = 2 * iq - 4 + i
            if kt < 0:
                continue
            steps.append((KT[64:128, kt * 128:(kt + 1) * 128], VA[:, kt, 1, :], None, wm[:, i, :], ["wm"]))
        attend(steps, slice(64, 128), 2)
        P.cp("pool", yb[:], y[:].rearrange("p g d -> p (g d)"), reads=["y"], writes=["yb"])
        P.dma("sp", yo_d[iq], yb[:], reads=["yb"], writes=["yo"], semkey="yb")
    P.build()
    return nc


def _nsa_maps(proj, inp, seq):
    NKT = seq // 128; NQB = NKT // 2; NCMP = seq // 16 - 1; NCT = (NCMP + 127) // 128
    half = 8
    inv = np.power(np.float32(500000.0), -np.arange(half, dtype=np.float32) * np.float32(2.0) / np.float32(16)).astype(np.float32)
    frow = np.zeros((1, 128), np.float32); srow = np.zeros((1, 128), np.float32); fpp = np.zeros((128, 2), np.float32)
    for base in (0, 64):
        frow[0, base:base + 8] = inv; frow[0, base + 8:base + 16] = inv
        srow[0, base:base + 8] = -1; srow[0, base + 8:base + 16] = 1
    fpp[:, 0] = frow[0]; fpp[:, 1] = srow[0]
    def swap64(a, axis):
        a = np.moveaxis(a, axis, -1)
        o = np.zeros_like(a)
        o[..., 0:8] = a[..., 8:16]; o[..., 8:16] = a[..., 0:8]
        return np.moveaxis(o, -1, axis)
    Fx = np.zeros((128, 8192), np.float32)
    Fx[np.arange(8192) // 64, np.arange(8192)] = 1
    p_ = np.arange(128)[:, None]; q_ = np.arange(128)[None, :]
    tri = (p_ <= q_).astype(np.float32); atri = (p_ > q_).astype(np.float32)
    ones = np.ones((128, 128), np.float32); zeros = np.zeros((128, 128), np.float32)
    NR = 0
    maps = []
    for c in range(8):
        b, kvh, par = c // 4, (c // 2) % 2, c % 2
        pn = proj[b, :seq, 1824:]
        pos = inp["positions"][b, :seq].astype(np.int32)
        def kvcol(i):
            o = 512 + i * 128 + kvh * 64
            return pn[:, o:o + 64]
        k_cmp, v_cmp, k_slc, v_slc, k_win, v_win = (kvcol(i) for i in range(6))
        kT = np.concatenate([k_slc.T, k_win.T], 0)
        kTs = np.concatenate([swap64(k_slc, 1).T, swap64(k_win, 1).T], 0)
        v = np.concatenate([v_slc, v_win], 1); vs = np.concatenate([swap64(v_slc, 1), swap64(v_win, 1)], 1)
        cmpT = np.concatenate([k_cmp.T, v_cmp.T], 0)
        post = pos.reshape(NKT, 128).T
        pc = np.zeros(NCT * 128, np.int32); pc[:NCMP] = pos[31::16][:NCMP]
        posc = pc.reshape(NCT, 128).T
        q = pn[:, kvh * 256:kvh * 256 + 256].reshape(seq, 4, 64)
        gate = pn[:, 512 + 768 + kvh * 12:512 + 768 + kvh * 12 + 12]
        own = np.arange(NQB) * 2 + par
        qb = q.reshape(NKT, 128, 4, 64)[own]
        qT1 = qb.transpose(0, 3, 2, 1).reshape(NQB, 64, 512)
        qTs1 = swap64(qb, 3).transpose(0, 3, 2, 1).reshape(NQB, 64, 512)
        qT = np.concatenate([qT1, qT1], 1); qTs = np.concatenate([qTs1, qTs1], 1)
        posq = np.tile(pos.reshape(NKT, 1, 128)[own], (1, 1, 4))
        gateb = gate.reshape(NKT, 128, 12)[own]
        w1 = np.concatenate([inp["wk1"][0].reshape(32, 64, 256).transpose(1, 0, 2), inp["wv1"][0].reshape(32, 64, 256).transpose(1, 0, 2)], 0)
        peT = np.concatenate([inp["pe_k"][0].T, inp["pe_v"][0].T], 0)
        b1 = np.concatenate([inp["bk1"][0].reshape(2, 128).T, inp["bv1"][0].reshape(2, 128).T], 1)
        w2 = np.concatenate([inp["wk2"][0].reshape(2, 128, 64).transpose(1, 0, 2), inp["wv2"][0].reshape(2, 128, 64).transpose(1, 0, 2)], 1)
        MC = np.zeros((128, 9, 128), np.float32)
        for m in range(8):
            MC[:, m, :] = (16 * p_ + 31 - 256 * m - 128 * par <= q_)
        MC[:, 8, :] = (16 * p_ + 31 - 2048 - 128 * par <= q_)
        mAB = np.stack([tri if par == 0 else ones, zeros if par == 0 else tri], 1)
        wlist = [atri, ones, ones, ones, tri, zeros] if par == 0 else [zeros, atri, ones, ones, ones, tri]
        wm = np.stack(wlist, 1)
        r_ = np.arange(32)[None, :]
        m32 = (16 * r_ - 225 <= 128 * par + p_).astype(np.float32)
        pat = np.zeros((128, 1024), np.float32)
        tl = np.arange(128)[:, None]; rr = np.arange(512)[None, :]
        cur = 256 + 2 * par + (tl >= 64)
        pat[:, 0:512] = (rr < cur - 1)
        pat[:, 512:1024] = np.where((rr == cur) | (rr == cur - 1), 1e6, np.where(rr > cur, -1e30, 0.0))
        c32 = lambda a: np.ascontiguousarray(a, dtype=np.float32)
        maps.append({"kT": c32(kT), "kTs": c32(kTs), "v": c32(v), "vs": c32(vs), "cmpT": c32(cmpT),
                     "posr": np.ascontiguousarray(pos[None, :]), "post": np.ascontiguousarray(post), "posc": np.ascontiguousarray(posc),
                     "qT": c32(qT), "qTs": c32(qTs), "posq": np.ascontiguousarray(posq), "gate": c32(gateb),
                     "w1": c32(w1), "peT": c32(peT), "b1": c32(b1), "w2": c32(w2),
                     "frow": frow, "srow": srow, "fpp": fpp, "idb": np.eye(128).astype(NPBF), "idf": np.eye(128, dtype=np.float32),
                     "Fx": Fx.astype(NPBF), "MC": MC.astype(NPBF), "mAB": mAB.astype(NPBF), "wm": wm.astype(NPBF),
                     "m32": m32, "pat": pat})
    return maps


def _rwkv_maps(proj, inp, seq):
    maps = []
    idn = np.eye(128, dtype=np.float32)
    blk = np.zeros((128, 128), np.float32); blk[:64, :64] = 1; blk[64:, 64:] = 1
    s_ = np.arange(64)[:, None]; t_ = np.arange(64)[None, :]
    strict = (s_ < t_).astype(np.float32); incl = (s_ <= t_).astype(np.float32)
    mg = np.tile(np.concatenate([strict, incl], 1), (1, 4))
    ma = np.tile((t_ < s_).astype(np.float32), (1, 2))
    for c in range(8):
        b, j = c // 4, c % 4
        hc = slice(128 * j, 128 * j + 128)
        cols = [slice(128 * j, 128 * j + 128), slice(512 + 128 * j, 512 + 128 * j + 128),
                slice(1024 + 128 * j, 1024 + 128 * j + 128), slice(1536, 1664), slice(1664, 1792), slice(1792, 1824)]
        cT = np.zeros((6, 128, seq + 1), np.float32); mu = np.zeros((128, 6), np.float32)
        for g, cs in enumerate(cols):
            n = cs.stop - cs.start
            cT[g, :n, 1:] = proj[b, :seq, cs].T
            mu[:n, g] = inp["mu_shift"][0, cs]
        vec = np.zeros((128, 8), np.float32)
        for i, k in enumerate(["w0", "a0", "k_k", "k_a"]):
            vec[:, i] = inp[k][0, hc]
        vec[:, 4] = inp["r_k"][0].reshape(-1)[hc]; vec[:, 5] = inp["lnx_w"][0, hc]; vec[:, 6] = inp["lnx_b"][0, hc]
        wa2 = np.concatenate([inp["w2"][0][:, hc], inp["a2"][0][:, hc]], 0)
        maps.append({"cT": cT, "mu": mu, "vec": vec, "wa2": np.ascontiguousarray(wa2),
                     "g2a": np.ascontiguousarray(inp["g2"][0][:128, hc]),
                     "g2b": np.ascontiguousarray(inp["g2"][0][128:160, hc]), "idn": idn, "blk": blk, "mg": mg, "ma": ma})
    return maps


def kernel(**inp):
    inp = {k: np.asarray(v) for k, v in inp.items()}
    x = inp["x"]
    B, S, D = x.shape
    cores = list(range(8))
    g = np.ascontiguousarray(inp["norm_mix"][0].reshape(8, 128).T)
    maps = []
    for c in cores:
        b, j = c // 4, c % 4
        xs = x[b, j * NTOK:(j + 1) * NTOK]
        maps.append({"xT": np.ascontiguousarray(xs.T), "x": np.ascontiguousarray(xs), "w": inp["w_in"][0], "g": g})
    res = run_bass_kernel_spmd(build_l1(), maps, core_ids=cores)
    proj = np.stack([r["proj"] for r in res.results]).reshape(B, S, IN_COLS)
    res = run_bass_kernel_spmd(build_l2(S), _rwkv_maps(proj, inp, S), core_ids=cores)
    yT = np.zeros((B, 1024, S), dtype=NPBF)
    for c in cores:
        b, j = c // 4, c % 4
        yT[b, 128 * j:128 * j + 128, :] = res.results[c]["yT"]
    res = run_bass_kernel_spmd(build_l3(S), _nsa_maps(proj, inp, S), core_ids=cores)
    for c in cores:
        b, kvh, par = c // 4, (c // 2) % 2, c % 2
        yo = res.results[c]["yo"]
        yT4 = yT[b, 512 + kvh * 256:512 + kvh * 256 + 256, :].reshape(256, S // 128, 128)
        yT4[:, par::2, :] = yo.transpose(2, 0, 1)
    maps = []
    idb = np.eye(128).astype(NPBF)
    for c in cores:
        b, j = c // 4, c % 4
        maps.append({"yT": np.ascontiguousarray(yT[b, :, j * NTOK:(j + 1) * NTOK]),
                     "x": np.ascontiguousarray(x[b, j * NTOK:(j + 1) * NTOK]),
                     "wo": inp["w_out"][0], "wg": inp["w_gate"][0], "wu": inp["w_up"][0], "wd": inp["w_down"][0],
                     "gf": inp["norm_ffn"], "gl": inp["norm_final"][None, :], "idn": idb})
    res = run_bass_kernel_spmd(build_l4(), maps, core_ids=cores)
    out = np.stack([r["out"] for r in res.results]).reshape(B, S, D)
    return out.astype(np.float32)
```
